# Optimizing a Trainium2 kernel written in Bass

```python
import math
import jax
import jax.numpy as jnp
from jax import lax
import numpy as np

D_MODEL = 1024
BATCH = 4
SEQ = 8192
DEPTH = 1

DA_HEADS = 8
DA_HEAD_DIM = 64
DA_V_DIM = 2 * DA_HEAD_DIM
DA_QK_WIDTH = DA_HEADS * 2 * DA_HEAD_DIM
DA_V_WIDTH = DA_HEADS * DA_V_DIM
DA_Q_BLOCK = 128
SSD_D_INNER = 2 * D_MODEL
SSD_HEAD_DIM = 64
SSD_HEADS = SSD_D_INNER // SSD_HEAD_DIM
SSD_GROUPS = 4
SSD_HEADS_PER_GROUP = SSD_HEADS // SSD_GROUPS
SSD_D_STATE = 128
SSD_BC_WIDTH = SSD_GROUPS * SSD_D_STATE
SSD_XBC_WIDTH = SSD_D_INNER + 2 * SSD_BC_WIDTH
SSD_CONV = 4
SSD_CHUNK = 256
DT_MIN = 0.001
DT_MAX = 0.1
N_BRANCHES = 2
IN_SPLITS = (DA_QK_WIDTH, DA_QK_WIDTH, DA_V_WIDTH, SSD_D_INNER, SSD_XBC_WIDTH, SSD_HEADS, N_BRANCHES * D_MODEL)
IN_WIDTH = sum(IN_SPLITS)
PEER_N_KEYS = 128
PEER_N_EXPERTS = PEER_N_KEYS * PEER_N_KEYS
PEER_HEADS = 8
PEER_TOPK = 16
PEER_QUERY_DIM = 256
PEER_HALF_DIM = PEER_QUERY_DIM // 2
PEER_TOKEN_BLOCK = 512
ALPHA = (2 * DEPTH) ** 0.25
BETA = (8 * DEPTH) ** -0.25
EPS = 1e-5

kernel_name = 'hybrid_diffattn_ssd_peer_block'


def layer_norm(x, g, b):
    xf = x.astype(jnp.float32)
    mu = jnp.mean(xf, axis=-1, keepdims=True)
    var = jnp.mean(jnp.square(xf - mu), axis=-1, keepdims=True)
    return ((xf - mu) * lax.rsqrt(var + EPS) * g + b).astype(x.dtype)


def rms_norm(x, w):
    xf = x.astype(jnp.float32)
    return (xf * lax.rsqrt(jnp.mean(jnp.square(xf), axis=-1, keepdims=True) + EPS) * w).astype(x.dtype)


def alibi_slopes(n):
    return jnp.asarray([2.0 ** (-8.0 * (i + 1) / n) for i in range(n)], jnp.float32)


def diff_attention(q, k, v, lam, lambda_init, subln_w):
    bsz, s = q.shape[:2]
    scale = DA_HEAD_DIM ** -0.5
    slopes = alibi_slopes(DA_HEADS)
    outs = []
    for i in range(s // DA_Q_BLOCK):
        q0 = i * DA_Q_BLOCK
        kend = q0 + DA_Q_BLOCK
        scores = jnp.einsum('bqhjd,bkhjd->bhjqk', q[:, q0:kend], k[:, :kend]).astype(jnp.float32) * scale
        dist = (jnp.arange(q0, kend)[:, None] - jnp.arange(kend)[None, :]).astype(jnp.float32)
        bias = jnp.where(dist[None] >= 0, -slopes[:, None, None] * dist[None], -jnp.inf)
        p = jax.nn.softmax(scores + bias[None, :, None], axis=-1)
        a = p[:, :, 0] - lam * p[:, :, 1]
        outs.append(jnp.einsum('bhqk,bkhe->bqhe', a.astype(v.dtype), v[:, :kend]))
    o = jnp.concatenate(outs, axis=1)
    o = rms_norm(o, subln_w) * (1.0 - lambda_init)
    return o.reshape(bsz, s, DA_V_WIDTH)


def causal_depthwise_conv(u, w, b):
    out = lax.conv_general_dilated(u, w[:, None, :].astype(u.dtype), window_strides=(1,),
                                   padding=[(SSD_CONV - 1, 0)], dimension_numbers=('NWC', 'WIO', 'NWC'),
                                   feature_group_count=u.shape[-1])
    return out + b


def ssd_chunked_scan(xh, dt, a, b_in, c_in):
    bsz, s = xh.shape[:2]
    pad = (-s) % SSD_CHUNK
    def padt(t):
        return jnp.pad(t, [(0, 0), (0, pad)] + [(0, 0)] * (t.ndim - 2))
    xh, dt, b_in, c_in = padt(xh), padt(dt), padt(b_in), padt(c_in)
    n_chunks = (s + pad) // SSD_CHUNK
    r = SSD_HEADS_PER_GROUP
    def chunks(t, tail):
        return t.reshape((bsz, n_chunks, SSD_CHUNK) + tail).swapaxes(0, 1)
    xs = (chunks(xh, (SSD_GROUPS, r, SSD_HEAD_DIM)), chunks(dt, (SSD_GROUPS, r)),
          chunks(b_in, (SSD_GROUPS, SSD_D_STATE)), chunks(c_in, (SSD_GROUPS, SSD_D_STATE)))
    a_gr = a.reshape(SSD_GROUPS, r)
    causal = jnp.tril(jnp.ones((SSD_CHUNK, SSD_CHUNK), dtype=bool))[None, :, :, None, None]

    def step(state, inp):
        xc, dtc, bc, cc = inp
        cum = jnp.cumsum(dtc * a_gr, axis=1)
        seg = cum[:, :, None] - cum[:, None, :]
        decay_ts = jnp.exp(jnp.where(causal, seg, -jnp.inf))
        cb = jnp.einsum('btgn,bsgn->btsg', cc, bc)
        w = cb[..., None] * decay_ts * dtc[:, None]
        y = jnp.einsum('btsgr,bsgrp->btgrp', w, xc)
        y = y + jnp.einsum('btgn,bgrpn->btgrp', cc, state) * jnp.exp(cum)[..., None]
        decay_end = jnp.exp(cum[:, -1:] - cum) * dtc
        state = state * jnp.exp(cum[:, -1])[..., None, None] + jnp.einsum('bsgn,bsgrp->bgrpn', bc, decay_end[..., None] * xc)
        return state, y

    state0 = jnp.zeros((bsz, SSD_GROUPS, r, SSD_HEAD_DIM, SSD_D_STATE), jnp.float32)
    _, ys = lax.scan(step, state0, xs)
    ys = ys.swapaxes(0, 1).reshape(bsz, n_chunks * SSD_CHUNK, SSD_HEADS, SSD_HEAD_DIM)
    return ys[:, :s]


def mamba2_branch(z, xbc, dt_raw, conv_w, conv_b, dt_bias, a_log, d_skip, norm_w):
    bsz, s = z.shape[:2]
    f32 = jnp.float32
    xbc = jax.nn.silu(causal_depthwise_conv(xbc, conv_w, conv_b))
    xs, b_in, c_in = jnp.split(xbc, [SSD_D_INNER, SSD_D_INNER + SSD_BC_WIDTH], axis=-1)
    xh = xs.reshape(bsz, s, SSD_HEADS, SSD_HEAD_DIM).astype(f32)
    dt = jax.nn.softplus(dt_raw.astype(f32) + dt_bias)
    a = -jnp.exp(a_log.astype(f32))
    y = ssd_chunked_scan(xh, dt, a,
                         b_in.reshape(bsz, s, SSD_GROUPS, SSD_D_STATE).astype(f32),
                         c_in.reshape(bsz, s, SSD_GROUPS, SSD_D_STATE).astype(f32))
    y = y + d_skip.astype(f32)[:, None] * xh
    y = y.reshape(bsz, s, SSD_D_INNER) * jax.nn.silu(z.astype(f32))
    y = rms_norm(y.reshape(bsz, s, SSD_GROUPS, SSD_D_INNER // SSD_GROUPS),
                 norm_w.reshape(SSD_GROUPS, -1)).reshape(bsz, s, SSD_D_INNER)
    return y.astype(z.dtype)


def peer_ffn(h, w_query, sub_keys, expert_u, expert_v):
    bsz, s, d = h.shape
    n_tok = bsz * s
    pad = (-n_tok) % PEER_TOKEN_BLOCK
    blocks = jnp.pad(h.reshape(n_tok, d), ((0, pad), (0, 0))).reshape(-1, PEER_TOKEN_BLOCK, d)

    def one_block(tb):
        q = (tb @ w_query).reshape(-1, PEER_HEADS, 2, PEER_HALF_DIM)
        sc = jnp.einsum('thjd,jkd->thjk', q, sub_keys).astype(jnp.float32)
        val, idx = lax.top_k(sc, PEER_TOPK)
        cand_s = val[:, :, 0, :, None] + val[:, :, 1, None, :]
        cand_i = idx[:, :, 0, :, None] * PEER_N_KEYS + idx[:, :, 1, None, :]
        cand_s = cand_s.reshape(cand_s.shape[0], PEER_HEADS, PEER_TOPK * PEER_TOPK)
        cand_i = cand_i.reshape(cand_i.shape[0], PEER_HEADS, PEER_TOPK * PEER_TOPK)
        top_s, pos = lax.top_k(cand_s, PEER_TOPK)
        eidx = jnp.take_along_axis(cand_i, pos, axis=-1)
        g = jax.nn.softmax(top_s, axis=-1)
        act = jax.nn.gelu(jnp.einsum('thkd,td->thk', expert_u[eidx], tb), approximate=False)
        wts = (g * act.astype(jnp.float32)).astype(tb.dtype)
        return jnp.einsum('thk,thkd->td', wts, expert_v[eidx])

    out = lax.map(one_block, blocks)
    return out.reshape(-1, d)[:n_tok].reshape(bsz, s, d)


def setup_inputs(seed: int = 0) -> dict:
    key = jax.random.key(seed)
    ks = jax.random.split(key, 28)
    f32 = jnp.float32
    L = DEPTH
    def nrm(k, shape, std):
        return jax.random.normal(k, shape, f32) * std
    dt0 = jnp.exp(jax.random.uniform(ks[7], (L, SSD_HEADS), f32) * (math.log(DT_MAX) - math.log(DT_MIN)) + math.log(DT_MIN))
    return {
        'x': nrm(ks[0], (BATCH, SEQ, D_MODEL), 1.0),
        'c': nrm(ks[1], (BATCH, D_MODEL), 1.0),
        'w_ada': nrm(ks[2], (L, D_MODEL, 6 * D_MODEL), 0.5 * D_MODEL ** -0.5),
        'b_ada': nrm(ks[3], (L, 6 * D_MODEL), 0.02),
        'w_in': nrm(ks[4], (L, D_MODEL, IN_WIDTH), D_MODEL ** -0.5),
        'conv_w': nrm(ks[5], (L, SSD_CONV, SSD_XBC_WIDTH), SSD_CONV ** -0.5),
        'conv_b': nrm(ks[6], (L, SSD_XBC_WIDTH), 0.02),
        'dt_bias': dt0 + jnp.log(-jnp.expm1(-dt0)),
        'a_log': jnp.log(jax.random.uniform(ks[8], (L, SSD_HEADS), f32, 1.0, 16.0)),
        'd_skip': 1.0 + nrm(ks[9], (L, SSD_HEADS), 0.1),
        'ssd_norm_w': 1.0 + nrm(ks[10], (L, SSD_D_INNER), 0.1),
        'lambda_q1': nrm(ks[11], (L, DA_HEAD_DIM), 0.1),
        'lambda_k1': nrm(ks[12], (L, DA_HEAD_DIM), 0.1),
        'lambda_q2': nrm(ks[13], (L, DA_HEAD_DIM), 0.1),
        'lambda_k2': nrm(ks[14], (L, DA_HEAD_DIM), 0.1),
        'da_subln_w': 1.0 + nrm(ks[15], (L, DA_V_DIM), 0.1),
        'w_attn_branch': nrm(ks[16], (L, DA_V_WIDTH, D_MODEL), DA_V_WIDTH ** -0.5),
        'w_ssd_branch': nrm(ks[17], (L, SSD_D_INNER, D_MODEL), SSD_D_INNER ** -0.5),
        'w_out': nrm(ks[18], (L, D_MODEL, D_MODEL), BETA * D_MODEL ** -0.5),
        'ln1_g': 1.0 + nrm(ks[19], (L, D_MODEL), 0.1),
        'ln1_b': nrm(ks[20], (L, D_MODEL), 0.02),
        'peer_w_query': nrm(ks[21], (L, D_MODEL, PEER_HEADS * PEER_QUERY_DIM), D_MODEL ** -0.5),
        'peer_sub_keys': nrm(ks[22], (L, 2, PEER_N_KEYS, PEER_HALF_DIM), PEER_HALF_DIM ** -0.5),
        'peer_u': nrm(ks[23], (L, PEER_N_EXPERTS, D_MODEL), D_MODEL ** -0.5),
        'peer_v': nrm(ks[24], (L, PEER_N_EXPERTS, D_MODEL), BETA * PEER_HEADS ** -0.5),
        'ln2_g': 1.0 + nrm(ks[25], (L, D_MODEL), 0.1),
        'ln2_b': nrm(ks[26], (L, D_MODEL), 0.02),
    }


def reference(x, c, w_ada, b_ada, w_in, conv_w, conv_b, dt_bias, a_log, d_skip, ssd_norm_w,
              lambda_q1, lambda_k1, lambda_q2, lambda_k2, da_subln_w, w_attn_branch, w_ssd_branch,
              w_out, ln1_g, ln1_b, peer_w_query, peer_sub_keys, peer_u, peer_v, ln2_g, ln2_b):
    bsz, s, _ = x.shape
    split_at = np.cumsum(IN_SPLITS)[:-1].tolist()
    f32 = jnp.float32
    for l in range(DEPTH):
        mod = jax.nn.silu(c) @ w_ada[l] + b_ada[l]
        shift1, scale1, gate1, shift2, scale2, gate2 = jnp.split(mod[:, None, :], 6, axis=-1)
        h = x * (1.0 + scale1) + shift1
        q, k, v, z, xbc, dt_raw, g_logits = jnp.split(h @ w_in[l], split_at, axis=-1)
        lambda_init = 0.8 - 0.6 * math.exp(-0.3 * l)
        lam = (jnp.exp(jnp.sum(lambda_q1[l].astype(f32) * lambda_k1[l].astype(f32)))
               - jnp.exp(jnp.sum(lambda_q2[l].astype(f32) * lambda_k2[l].astype(f32))) + lambda_init)
        y_attn = diff_attention(q.reshape(bsz, s, DA_HEADS, 2, DA_HEAD_DIM),
                                k.reshape(bsz, s, DA_HEADS, 2, DA_HEAD_DIM),
                                v.reshape(bsz, s, DA_HEADS, DA_V_DIM),
                                lam, lambda_init, da_subln_w[l]) @ w_attn_branch[l]
        y_ssd = mamba2_branch(z, xbc, dt_raw, conv_w[l], conv_b[l], dt_bias[l], a_log[l],
                              d_skip[l], ssd_norm_w[l]) @ w_ssd_branch[l]
        g_attn, g_ssd = jnp.split(jax.nn.sigmoid(g_logits), N_BRANCHES, axis=-1)
        mixed = (g_attn * y_attn + g_ssd * y_ssd) @ w_out[l]
        x = layer_norm(ALPHA * x + gate1 * mixed, ln1_g[l], ln1_b[l])
        h2 = x * (1.0 + scale2) + shift2
        y_ffn = peer_ffn(h2, peer_w_query[l], peer_sub_keys[l], peer_u[l], peer_v[l])
        x = layer_norm(ALPHA * x + gate2 * y_ffn, ln2_g[l], ln2_b[l])
    return x
```

```python
import math
import numpy as np
import ml_dtypes
import concourse.bass as bass
import concourse.mybir as mybir
from concourse.bass_utils import run_bass_kernel_spmd

F32 = mybir.dt.float32
BF16 = mybir.dt.bfloat16
U32 = mybir.dt.uint32
AF = mybir.ActivationFunctionType
ALU = mybir.AluOpType
AX = mybir.AxisListType

D = 1024
KC = 8
EPS = 1e-5
ALPHA = 2.0 ** 0.25
LAMBDA_INIT = 0.8 - 0.6 * math.exp(0.0)
NEG = -30000.0


class _Op:
    __slots__ = ("eng", "fn", "deps", "dma_key", "needed", "val", "idx")


class Prog:
    ENGS = ["pe", "act", "dve", "pool", "sp"]

    def __init__(self, nc):
        self.nc = nc
        self.streams = {e: [] for e in self.ENGS}
        self.lw = {}
        self.rd = {}
        self.dma_keys = []
        self.n_ops = 0
        self.bar = {e: [] for e in self.ENGS}
        self.ukey = 0

    def barrier(self):
        self.ukey = 0
        lasts = []
        for e in self.ENGS:
            for o in reversed(self.streams[e]):
                if o.dma_key is None:
                    lasts.append(o)
                    break
        seen = set()
        for e in self.ENGS:
            for o in reversed(self.streams[e]):
                if o.dma_key is not None and o.dma_key not in seen:
                    seen.add(o.dma_key)
                    lasts.append(o)
        for o in lasts:
            o.needed = True
        for e in self.ENGS:
            self.bar[e] = list(lasts)
        self.lw = {}
        self.rd = {}

    def op(self, eng, fn, reads=(), writes=(), dma_key=None):
        o = _Op()
        o.eng = eng
        o.fn = fn
        o.dma_key = dma_key
        o.needed = dma_key is not None
        o.val = None
        o.idx = self.n_ops
        self.n_ops += 1
        deps = list(self.bar[eng])
        self.bar[eng] = []
        for t in reads:
            w = self.lw.get(t)
            if w is not None:
                deps.append(w)
        for t in writes:
            w = self.lw.get(t)
            if w is not None:
                deps.append(w)
            deps.extend(self.rd.get(t, ()))
        best = {}
        for d in deps:
            if d.dma_key is None and d.eng == "pe" and eng == "pe" and dma_key is None:
                continue
            k = ("dma", d.dma_key) if d.dma_key is not None else ("eng", d.eng)
            if k not in best or best[k].idx < d.idx:
                best[k] = d
        ds = list(best.values())
        for d in ds:
            d.needed = True
        o.deps = ds
        for t in writes:
            self.lw[t] = o
            self.rd[t] = []
        for t in reads:
            self.rd.setdefault(t, []).append(o)
        if dma_key is not None and dma_key not in self.dma_keys:
            self.dma_keys.append(dma_key)
        self.streams[eng].append(o)
        return o

    def dma(self, out, in_, reads=(), writes=(), key=None, eng="sp", **kw):
        if key is None:
            key = "u%d" % self.ukey
            self.ukey += 1
        key = (key, eng)
        return self.op(eng, lambda e: e.dma_start(out=out, in_=in_, **kw), reads, writes, dma_key=key)

    def emit(self):
        nc = self.nc
        sems = {}
        for e in self.ENGS:
            sems[("eng", e)] = nc.alloc_semaphore("sem_" + e)
        for k in self.dma_keys:
            sems[("dma", k)] = nc.alloc_semaphore("semd_" + str(k))
        for e in self.ENGS:
            c = 0
            for o in self.streams[e]:
                if o.dma_key is None and o.needed:
                    c += 1
                    o.val = c
        dcnt = {k: 0 for k in self.dma_keys}
        allops = sorted([o for e in self.ENGS for o in self.streams[e]], key=lambda o: o.idx)
        for o in allops:
            if o.dma_key is not None:
                dcnt[o.dma_key] += 1
                o.val = dcnt[o.dma_key]
        final = dict(dcnt)

        def run(e, handle):
            waited = {}
            for o in self.streams[e]:
                need = {}
                for d in o.deps:
                    if d.dma_key is not None:
                        k = ("dma", d.dma_key)
                        v = d.val * 16
                    else:
                        k = ("eng", d.eng)
                        v = d.val
                    if need.get(k, 0) < v:
                        need[k] = v
                for k, v in need.items():
                    if waited.get(k, 0) >= v:
                        continue
                    handle.wait_ge(sems[k], v)
                    waited[k] = v
                inst = o.fn(handle)
                if o.dma_key is not None:
                    inst.then_inc(sems[("dma", o.dma_key)], 16)
                elif o.needed:
                    inst.then_inc(sems[("eng", e)], 1)
            if e == "sp":
                for k, v in final.items():
                    if v > 0:
                        handle.wait_ge(sems[("dma", k)], v * 16)

        with nc.Block() as block:
            @block.tensor
            def _(h):
                run("pe", h)

            @block.scalar
            def _(h):
                run("act", h)

            @block.vector
            def _(h):
                run("dve", h)

            @block.gpsimd
            def _(h):
                run("pool", h)

            @block.sync
            def _(h):
                run("sp", h)


def _dsz(dt):
    return 2 if dt == BF16 else 4


class Arena:
    def __init__(self, base, nbytes):
        self.base = base
        self.nbytes = nbytes
        self.off = 0

    def reset(self, off=0):
        self.off = off

    def alloc(self, shape, dt):
        n = 1
        for s in shape[1:]:
            n *= s
        nb = n * _dsz(dt)
        nb_al = (nb + 63) // 64 * 64
        assert self.off + nb_al <= self.nbytes, ("arena overflow", self.off, nb_al, self.nbytes)
        v = self.base[:, self.off // 2:(self.off + nb) // 2]
        self.off += nb_al
        if dt != BF16:
            v = v.bitcast(dt)
        if shape[0] != 128:
            v = v[0:shape[0]]
        if len(shape) == 3:
            v = v.rearrange("p (a b) -> p a b", a=shape[1])
        elif len(shape) == 4:
            v = v.rearrange("p (a b c) -> p a b c", a=shape[1], b=shape[2])
        return v


def MM(P, out, lhsT, rhs, start, stop, r, w, **kw):
    return P.op("pe", lambda e: e.matmul(out, lhsT=lhsT, rhs=rhs, start=start, stop=stop, **kw), r, w)


def TR(P, out, in_, ident, r, w):
    return P.op("pe", lambda e: e.transpose(out=out, in_=in_, identity=ident), r, w)


def ACT(P, out, in_, func, r, w, **kw):
    return P.op("act", lambda e: e.activation(out=out, in_=in_, func=func, **kw), r, w)


def CP(P, eng, out, in_, r, w):
    if eng == "act":
        return P.op("act", lambda e: e.copy(out=out, in_=in_), r, w)
    return P.op(eng, lambda e: e.tensor_copy(out=out, in_=in_), r, w)


def TT(P, eng, out, in0, in1, op, r, w):
    return P.op(eng, lambda e: e.tensor_tensor(out=out, in0=in0, in1=in1, op=op), r, w)


def TS(P, eng, out, in0, s1, s2, op0, op1, r, w):
    if s2 is None:
        return P.op(eng, lambda e: e.tensor_scalar(out=out, in0=in0, scalar1=s1, scalar2=None, op0=op0), r, w)
    return P.op(eng, lambda e: e.tensor_scalar(out=out, in0=in0, scalar1=s1, scalar2=s2, op0=op0, op1=op1), r, w)


def STT(P, eng, out, in0, scalar, in1, op0, op1, r, w):
    return P.op(eng, lambda e: e.scalar_tensor_tensor(out=out, in0=in0, scalar=scalar, in1=in1, op0=op0, op1=op1), r, w)


def MEMSET(P, eng, ap, val, r, w):
    return P.op(eng, lambda e: e.memset(ap, val), r, w)


def RSUM(P, eng, out, in_, r, w):
    return P.op(eng, lambda e: e.tensor_reduce(out=out, in_=in_, axis=AX.X, op=ALU.add), r, w)


def RSQRT(P, out, in_, scale, r, w):
    P.op("dve", lambda e: e.tensor_scalar(out=out, in0=in_, scalar1=scale, scalar2=EPS, op0=ALU.mult, op1=ALU.add), r, w)
    P.op("act", lambda e: e.activation(out=out, in_=out, func=AF.Ln), w, w)
    P.op("act", lambda e: e.activation(out=out, in_=out, func=AF.Exp, scale=-0.5), w, w)


class Ctx:
    pass


IN_SHAPES = None


def build(TP, TO, dbg=None):
    dbg = dbg or set()
    TK = TP + TO
    NKB = TK // 128
    NQT = TO // 512
    ND = NKB + 4
    nc = bass.Bass("TRN2", target_bir_lowering=False)
    P = Prog(nc)
    C = Ctx()
    C.nc, C.P, C.TP, C.TO, C.TK, C.NKB, C.NQT, C.ND = nc, P, TP, TO, TK, NKB, NQT, ND

    def din(name, shape, dt=F32):
        return nc.dram_tensor(name, list(shape), dt, kind="ExternalInput").ap()

    def dscr(name, shape, dt):
        return nc.dram_tensor(name, list(shape), dt, kind="Internal").ap()

    I = {}
    I["xp"] = din("xp", [TP, D])
    I["xo"] = din("xo", [TO, D])
    I["flag"] = din("flag", [128, 1])
    I["c_col"] = din("c_col", [128, 8])
    I["w_ada"] = din("w_ada", [D, 6 * D])
    I["b_ada"] = din("b_ada", [1, 6 * D])
    I["w_in"] = din("w_in", [D, 10272])
    I["conv_w"] = din("conv_w", [4, 3072])
    I["conv_b"] = din("conv_b", [1, 3072])
    I["dt_bias"] = din("dt_bias", [1, 32])
    I["a_log"] = din("a_log", [1, 32])
    I["d_skip"] = din("d_skip", [1, 32])
    I["ssd_norm_w"] = din("ssd_norm_w", [1, 2048])
    for nm in ("lambda_q1", "lambda_k1", "lambda_q2", "lambda_k2"):
        I[nm] = din(nm, [1, 64])
    I["da_subln_w"] = din("da_subln_w", [1, 128])
    I["w_attn_branch"] = din("w_attn_branch", [1024, 1024])
    I["w_ssd_branch"] = din("w_ssd_branch", [2048, 1024])
    I["w_out"] = din("w_out", [1024, 1024])
    I["ln1_g"] = din("ln1_g", [1, D])
    I["ln1_b"] = din("ln1_b", [1, D])
    I["peer_w_query"] = din("peer_w_query", [D, 2048])
    I["peer_sub_keys"] = din("peer_sub_keys", [2, 128, 128])
    C.full = not any(k_.startswith("stop") for k_ in dbg)
    NEXP = 16384 if C.full else 128
    I["peer_u"] = din("peer_u", [NEXP, D])
    I["peer_v"] = din("peer_v", [NEXP, D])
    I["ln2_g"] = din("ln2_g", [1, D])
    I["ln2_b"] = din("ln2_b", [1, D])
    I["k_ident"] = din("k_ident", [128, 128])
    I["k_tri"] = din("k_tri", [128, 128])
    I["k_U"] = din("k_U", [128, 2, 256])
    I["k_maskneg"] = din("k_maskneg", [128, 256])
    I["k_oh"] = din("k_oh", [32, 32, 128])
    I["k_aug"] = din("k_aug", [8, 2, 512], BF16)
    I["k_abias"] = din("k_abias", [8, 128, ND])
    I["k_iota"] = din("k_iota", [128, 128])
    C.I = I
    out = nc.dram_tensor("out", [TO, D], F32, kind="ExternalOutput").ap()
    C.out = out
    C.dbg = {}
    C.dbgreq = dbg if isinstance(dbg, dict) else {}

    S = {}
    S["mod"] = dscr("s_mod", [1, 6 * D], F32)
    S["qT"] = dscr("s_qT", [8, 128, TO], BF16)
    S["kT"] = dscr("s_kT", [8, 128, TK], BF16)
    S["v"] = dscr("s_v", [TK, 1024], BF16)
    S["z"] = dscr("s_z", [TO, 2048], F32)
    S["g"] = dscr("s_g", [TO, 2048], F32)
    S["xbcT"] = dscr("s_xbcT", [3072, TK], BF16)
    S["dt"] = dscr("s_dt", [TK, 32], F32)
    S["o"] = dscr("s_o", [TO, 1024], BF16)
    S["yssd"] = dscr("s_yssd", [TO, 2048], BF16)
    S["x1"] = dscr("s_x1", [TO, D], F32)
    S["h2T"] = dscr("s_h2T", [128, 8, TO], BF16)
    if C.full:
        S["UT"] = dscr("s_UT", [128, 128, 8, 128], BF16)
        S["VB"] = dscr("s_VB", [128, 128, 1024], BF16)
    C.S = S

    def sb(name, shape, dt):
        return nc.alloc_sbuf_tensor(name, list(shape), dt).ap()

    C.modb = sb("modb", [128, 6 * D], F32)
    C.identf = sb("identf", [128, 128], F32)
    C.identb = sb("identb", [128, 128], BF16)
    C.trif = sb("trif", [128, 128], F32)
    C.trib = sb("trib", [128, 128], BF16)
    C.onesf = sb("onesf", [128, 128], F32)
    C.flag = sb("flag_sb", [128, 1], F32)
    C.halo = sb("halo", [128, 24, 4], BF16)
    C.cwT = sb("cwT", [128, 24, 4], F32)
    C.cwlo = sb("cwlo", [128, 24, 4], F32)
    C.cwhb = sb("cwhb", [128, 24, 4], BF16)
    C.cbT = sb("cbT", [128, 24], F32)
    ARENA_BYTES = 172 * 1024
    C.A = Arena(sb("arena", [128, ARENA_BYTES // 2], BF16), ARENA_BYTES)
    C.PS = [nc.alloc_psum_tensor("ps%d" % i, [128, 512], F32).ap() for i in range(8)]

    P.dma(C.identf, I["k_ident"], writes=["identf"])
    P.dma(C.trif, I["k_tri"], writes=["trif"])
    P.dma(C.flag, I["flag"], writes=["flag"])
    CP(P, "dve", C.identb, C.identf, ["identf"], ["identb"])
    CP(P, "dve", C.trib, C.trif, ["trif"], ["trib"])
    MEMSET(P, "dve", C.onesf, 1.0, [], ["onesf"])

    phase_mod(C)
    P.barrier()
    phase_inproj(C, "prefix")
    P.barrier()
    phase_inproj(C, "own")
    P.barrier()
    phase_attn(C)
    P.barrier()
    if "stop_attn" not in dbg:
        phase_ssd(C)
        P.barrier()
        if "stop_ssd" not in dbg:
            phase_merge(C)
            P.barrier()
            if "stop_merge" not in dbg:
                phase_peer(C)
    P.barrier()
    for nm in [k_ for k_ in C.dbgreq if not k_.startswith("stop")]:
        src_ap = C.S[nm]
        d = nc.dram_tensor("dbg_" + nm, list(src_ap.shape), src_ap.dtype, kind="ExternalOutput").ap()
        P.dma(d, src_ap)
    P.emit()
    return nc


def phase_mod(C):
    P, A, I, PS = C.P, C.A, C.I, C.PS
    A.reset()
    ccol = A.alloc([128, 8], F32)
    sc = A.alloc([128, 8], F32)
    brow = A.alloc([1, 6 * D], F32)
    mrow = A.alloc([1, 6 * D], F32)
    wst = [A.alloc([128, 8, 512], F32) for _ in range(2)]
    P.dma(ccol, I["c_col"], writes=["ccol"])
    P.dma(brow, I["b_ada"], writes=["brow"])
    ACT(P, sc, ccol, AF.Silu, ["ccol"], ["sc"])
    wv = I["w_ada"].rearrange("(c p) n -> p c n", p=128)
    for nt in range(12):
        s = nt % 2
        P.dma(wst[s], wv[:, :, nt * 512:(nt + 1) * 512], writes=[("wst", s)], key=("w", s))
        ps = PS[s][0:1, :]
        for c in range(8):
            MM(P, ps, sc[:, c:c + 1], wst[s][:, c, :], c == 0, c == 7, ["sc", ("wst", s)], [("ps", s)])
        TT(P, "dve", mrow[:, nt * 512:(nt + 1) * 512], ps, brow[:, nt * 512:(nt + 1) * 512], ALU.add,
           [("ps", s), "brow"], ["mrow"])
    for k in (1, 4):
        TS(P, "dve", mrow[:, k * D:(k + 1) * D], mrow[:, k * D:(k + 1) * D], 1.0, None, ALU.add, None, ["mrow"], ["mrow"])
    cwr = A.alloc([4, 3072], F32)
    cbr = A.alloc([1, 3072], F32)
    P.dma(cwr, I["conv_w"], writes=["cwr"])
    P.dma(cbr, I["conv_b"], writes=["cbr"])
    for ci in range(24):
        TR(P, PS[6][:, ci * 4:ci * 4 + 4], cwr[:, ci * 128:(ci + 1) * 128], C.identf[0:4, 0:4], ["cwr", "identf"], [("ps", 6)])
        TR(P, PS[7][:, ci:ci + 1], cbr[:, ci * 128:(ci + 1) * 128], C.identf[0:1, 0:1], ["cbr", "identf"], [("ps", 7)])
    CP(P, "dve", C.cwT.rearrange("p c k -> p (c k)"), PS[6][:, 0:96], [("ps", 6)], ["cwT"])
    CP(P, "dve", C.cbT, PS[7][:, 0:24], [("ps", 7)], ["cbT"])
    CP(P, "dve", C.cwhb, C.cwT, ["cwT"], ["cwhb"])
    CP(P, "dve", C.cwlo, C.cwhb, ["cwhb"], ["cwlo"])
    TT(P, "dve", C.cwlo, C.cwT, C.cwlo, ALU.subtract, ["cwT", "cwlo"], ["cwlo"])
    P.dma(C.S["mod"], mrow, reads=["mrow"], writes=["s_mod"])
    P.dma(C.modb, C.S["mod"].to_broadcast([128, 6 * D]), reads=["s_mod"], writes=["modb"])


def phase_inproj(C, which):
    P, A, I, PS, S = C.P, C.A, C.I, C.PS, C.S
    TP, TO = C.TP, C.TO
    A.reset()
    T = TP if which == "prefix" else TO
    toff = 0 if which == "prefix" else TP
    xsrc = I["xp"] if which == "prefix" else I["xo"]
    hT = A.alloc([128, 8, T], BF16)
    xs = [A.alloc([128, D], F32) for _ in range(2)]
    ht = [A.alloc([128, D], F32) for _ in range(2)]
    hb = [A.alloc([128, D], BF16) for _ in range(2)]
    wst = [A.alloc([128, 8, 512], F32) for _ in range(2)]
    wbf = [A.alloc([128, 8, 512], BF16) for _ in range(2)]
    ev = [A.alloc([128, 512], F32) for _ in range(4)]
    evb = [A.alloc([128, 512], BF16) for _ in range(4)]
    preb = [A.alloc([128, 516], BF16) for _ in range(2)]
    diag = [A.alloc([128, 8, 128], BF16) for _ in range(2)]
    for tt in range(T // 128):
        s = tt % 2
        P.dma(xs[s], xsrc[tt * 128:(tt + 1) * 128, :], writes=[("xs", s)], key=("x", s))
        TT(P, "dve", ht[s], xs[s], C.modb[:, D:2 * D], ALU.mult, [("xs", s), "modb"], [("ht", s)])
        TT(P, "dve", hb[s], ht[s], C.modb[:, 0:D], ALU.add, [("ht", s), "modb"], [("hb", s)])
        psb = PS[s].bitcast(BF16)
        for c in range(8):
            TR(P, psb[:, c * 128:(c + 1) * 128], hb[s][:, c * 128:(c + 1) * 128], C.identb,
               [("hb", s), "identb"], [("ps", s)])
        CP(P, "act", hT[:, :, tt * 128:(tt + 1) * 128], psb.rearrange("p (c t) -> p c t", c=8),
           [("ps", s)], ["hT"])
    tiles = []
    own = which == "own"
    if own:
        for h in range(8):
            tiles.append(("F", h * 128, 128, ("q", h)))
    for h in range(8):
        tiles.append(("F", 1024 + h * 128, 128, ("k", h)))
    for i in range(2):
        tiles.append(("T", 2048 + i * 512, 512, ("v", i)))
    if own:
        for i in range(4):
            tiles.append(("T", 3072 + i * 512, 512, ("z", i)))
    for i in range(24):
        tiles.append(("X", 5120 + i * 128, 128, ("xbc", i)))
    tiles.append(("T", 8192, 32, ("dt", 0)))
    if own:
        for i in range(4):
            tiles.append(("T", 8224 + i * 512, 512, ("g", i)))
    wv = I["w_in"].rearrange("(c p) n -> p c n", p=128)

    def load_w(ti):
        kind, c0, ncol, _ = tiles[ti]
        s = ti % 2
        P.dma(wst[s][:, :, 0:ncol], wv[:, :, c0:c0 + ncol], writes=[("wst", s)], key=("w", s))
        CP(P, "dve", wbf[s][:, :, 0:ncol], wst[s][:, :, 0:ncol], [("wst", s)], [("wbf", s)])

    load_w(0)
    cnt = 0
    for ti, (kind, c0, ncol, dest) in enumerate(tiles):
        if ti + 1 < len(tiles):
            load_w(ti + 1)
        s = ti % 2
        dk, di = dest
        if kind == "X":
            ci = di
            ds_ = ci % 2
            for k in range(4):
                TS(P, "dve", diag[ds_][:, 2 * k, :], C.identb, C.cwT[:, ci, k:k + 1], None, ALU.mult, None,
                   ["identb", "cwT"], [("diag", ds_)])
                TS(P, "dve", diag[ds_][:, 2 * k + 1, :], C.identb, C.cwlo[:, ci, k:k + 1], None, ALU.mult, None,
                   ["identb", "cwlo"], [("diag", ds_)])
            ntg = T // 512
            for tg in range(ntg):
                b = 2 + cnt % 4
                e = cnt % 4
                pb = cnt % 2
                cb_ = 6 + cnt % 2
                cnt += 1
                for c in range(8):
                    MM(P, PS[b], wbf[s][:, c, 0:128], hT[:, c, tg * 512:(tg + 1) * 512], c == 0, c == 7,
                       [("wbf", s), "hT"], [("ps", b)])
                if own:
                    CP(P, "dve", preb[pb][:, 3:515], PS[b], [("ps", b)], [("preb", pb)])
                else:
                    TS(P, "dve", preb[pb][:, 3:515], PS[b], C.flag, None, ALU.mult, None, [("ps", b), "flag"], [("preb", pb)])
                if tg == 0:
                    if own:
                        CP(P, "dve", preb[pb][:, 0:3], C.halo[:, ci, 0:3], [("halo", ci)], [("preb", pb)])
                    else:
                        MEMSET(P, "dve", preb[pb][:, 0:3], 0.0, [], [("preb", pb)])
                else:
                    CP(P, "dve", preb[pb][:, 0:3], preb[1 - pb][:, 512:515], [("preb", 1 - pb)], [("preb", pb)])
                if (not own) and tg == ntg - 1:
                    CP(P, "dve", C.halo[:, ci, 0:3], preb[pb][:, 512:515], [("preb", pb)], [("halo", ci)])
                for k in range(4):
                    for hl in range(2):
                        MM(P, PS[cb_], diag[ds_][:, 2 * k + hl, :], preb[pb][:, k:k + 512], (k == 0 and hl == 0), (k == 3 and hl == 1),
                           [("diag", ds_), ("preb", pb)], [("ps", cb_)])
                ACT(P, evb[e], PS[cb_], AF.Silu, [("ps", cb_), "cbT"], [("evb", e)], bias=C.cbT[:, ci:ci + 1])
                P.dma(S["xbcT"][ci * 128:(ci + 1) * 128, toff + tg * 512: toff + (tg + 1) * 512], evb[e],
                      reads=[("evb", e)], key=("se", e), eng="act")
        elif kind == "F":
            for tg in range(T // 512):
                b = 2 + cnt % 4
                e = cnt % 4
                cnt += 1
                for c in range(8):
                    MM(P, PS[b], wbf[s][:, c, 0:128], hT[:, c, tg * 512:(tg + 1) * 512], c == 0, c == 7,
                       [("wbf", s), "hT"], [("ps", b)])
                if dk == "q":
                    ACT(P, evb[e], PS[b], AF.Identity, [("ps", b)], [("evb", e)], scale=0.125)
                    P.dma(S["qT"][di][:, tg * 512:(tg + 1) * 512], evb[e], reads=[("evb", e)], key=("se", e), eng="act")
                else:
                    CP(P, "dve", evb[e], PS[b], [("ps", b)], [("evb", e)])
                    P.dma(S["kT"][di][:, toff + tg * 512: toff + (tg + 1) * 512], evb[e], reads=[("evb", e)], key=("se", e), eng="sp")
        else:
            for tt in range(T // 128):
                b = 2 + cnt % 4
                e = cnt % 4
                cnt += 1
                r0 = toff + tt * 128
                for c in range(8):
                    MM(P, PS[b][:, 0:ncol], hT[:, c, tt * 128:(tt + 1) * 128], wbf[s][:, c, 0:ncol], c == 0, c == 7,
                       [("wbf", s), "hT"], [("ps", b)])
                if dk == "v":
                    dst = S["v"][r0:r0 + 128, di * 512:(di + 1) * 512]
                    if own:
                        if e % 2 == 0:
                            CP(P, "act", evb[e], PS[b], [("ps", b)], [("evb", e)])
                        else:
                            CP(P, "dve", evb[e], PS[b], [("ps", b)], [("evb", e)])
                    else:
                        if e % 2 == 0:
                            ACT(P, evb[e], PS[b], AF.Identity, [("ps", b), "flag"], [("evb", e)], scale=C.flag)
                        else:
                            TS(P, "dve", evb[e], PS[b], C.flag, None, ALU.mult, None, [("ps", b), "flag"], [("evb", e)])
                    P.dma(dst, evb[e], reads=[("evb", e)], key=("se", e), eng=("act" if e % 2 == 0 else "sp"))
                elif dk == "z":
                    if e % 2 == 0:
                        CP(P, "act", ev[e], PS[b], [("ps", b)], [("ev", e)])
                    else:
                        CP(P, "dve", ev[e], PS[b], [("ps", b)], [("ev", e)])
                    P.dma(S["z"][tt * 128:(tt + 1) * 128, di * 512:(di + 1) * 512], ev[e], reads=[("ev", e)], key=("sf", e),
                          eng=("act" if e % 2 == 0 else "sp"))
                elif dk == "g":
                    ACT(P, ev[e], PS[b], AF.Sigmoid, [("ps", b)], [("ev", e)])
                    P.dma(S["g"][tt * 128:(tt + 1) * 128, di * 512:(di + 1) * 512], ev[e], reads=[("ev", e)], key=("sf", e), eng="act")
                elif dk == "dt":
                    CP(P, "dve", ev[e][:, 0:32], PS[b][:, 0:32], [("ps", b)], [("ev", e)])
                    P.dma(S["dt"][r0:r0 + 128, :], ev[e][:, 0:32], reads=[("ev", e)], key=("sf", e), eng="sp")


def phase_attn(C):
    P, A, I, PS, S = C.P, C.A, C.I, C.PS, C.S
    TP, TO, TK, NKB, NQT, ND = C.TP, C.TO, C.TK, C.NKB, C.NQT, C.ND
    A.reset()
    KA = [A.alloc([66, TK], BF16) for _ in range(2)]
    QA = [A.alloc([66, TO], BF16) for _ in range(2)]
    VP = A.alloc([128, NKB, 132], BF16)
    abias = A.alloc([128, ND], F32)
    NE = 6
    SB = [0, 1, 6, 7]
    Eb = [A.alloc([128, 512], BF16) for _ in range(NE)]
    sublnb = A.alloc([128, 128], F32)
    lam4 = A.alloc([128, 4, 64], F32)
    lamt = A.alloc([128, 2, 64], F32)
    lams = A.alloc([128, 2], F32)
    neglam = A.alloc([128, 1], F32)
    rz = [A.alloc([128, 2], F32) for _ in range(2)]
    t1 = [A.alloc([128, 128], F32) for _ in range(2)]
    ot = [A.alloc([128, 128], F32) for _ in range(2)]
    sq = [A.alloc([128, 128], F32) for _ in range(2)]
    ss = [A.alloc([128, 1], F32) for _ in range(2)]
    ob = [A.alloc([128, 128], BF16) for _ in range(2)]
    for j in range(2):
        MEMSET(P, "dve", KA[j][64:66, :], 1.0, [], [("KAaug", j)])
    MEMSET(P, "dve", VP[:, :, 128:129], 1.0, [], ["VPone"])
    if TP > 0:
        TS(P, "dve", VP[:, 0:TP // 128, 128:129], VP[:, 0:TP // 128, 128:129], C.flag, None, ALU.mult, None,
           ["VPone", "flag"], ["VPone"])
    P.dma(sublnb, I["da_subln_w"].to_broadcast([128, 128]), writes=["sublnb"])
    TS(P, "dve", sublnb, sublnb, 1.0 - LAMBDA_INIT, None, ALU.mult, None, ["sublnb"], ["sublnb"])
    for i, nm in enumerate(("lambda_q1", "lambda_k1", "lambda_q2", "lambda_k2")):
        P.dma(lam4[:, i, :], I[nm].to_broadcast([128, 64]), writes=[("lam4", i)])
    TT(P, "dve", lamt[:, 0, :], lam4[:, 0, :], lam4[:, 1, :], ALU.mult, [("lam4", 0), ("lam4", 1)], [("lamt", 0)])
    TT(P, "dve", lamt[:, 1, :], lam4[:, 2, :], lam4[:, 3, :], ALU.mult, [("lam4", 2), ("lam4", 3)], [("lamt", 1)])
    RSUM(P, "dve", lams, lamt, [("lamt", 0), ("lamt", 1)], ["lams"])
    ACT(P, lams, lams, AF.Exp, ["lams"], ["lams"])
    TT(P, "dve", neglam, lams[:, 1:2], lams[:, 0:1], ALU.subtract, ["lams"], ["neglam"])
    TS(P, "dve", neglam, neglam, -LAMBDA_INIT, None, ALU.add, None, ["neglam"], ["neglam"])

    def acc_ap(j, u):
        b = 2 + j * 2 + u // 2
        return b, PS[b][:, (u % 2) * 256:(u % 2) * 256 + 129]

    ecnt = 0
    scnt = 0
    pcnt = 0
    for h in range(8):
        for j in range(2):
            P.dma(KA[j][0:64, :], S["kT"][h][j * 64:(j + 1) * 64, :], writes=[("KA", j)], key=("KA", j))
            P.dma(QA[j][0:64, :], S["qT"][h][j * 64:(j + 1) * 64, :], writes=[("QA", j)], key=("QA", j))
            P.dma(QA[j][64:66, :].rearrange("p (t q) -> p t q", q=512),
                  I["k_aug"][h].unsqueeze(1).to_broadcast([2, NQT, 512]), writes=[("QA", j)], key=("QA", j))
        vsrc = S["v"][:, h * 128:(h + 1) * 128].rearrange("(kb p) e -> p kb e", p=128)
        for k0 in range(0, NKB, 8):
            k1 = min(NKB, k0 + 8)
            P.dma(VP[:, k0:k1, 0:128], vsrc[:, k0:k1, :], writes=["VP"], key="VP")
        P.dma(abias, I["k_abias"][h], writes=["abias"], key="abias")
        for qt in range(NQT):
            q0 = TP + 512 * qt
            qb = q0 // 128
            nkb = qb + 4
            items = [(kb, j) for kb in range(nkb) for j in range(2)]
            slots = {}

            def emit_S(i):
                nonlocal scnt, ecnt
                kb, j = items[i]
                u0 = max(0, kb - qb)
                c0 = u0 * 128
                di = kb - qb + (NKB - 4)
                sb_ = SB[scnt % 4]
                scnt += 1
                eb = ecnt % NE
                ecnt += 1
                slots[i] = eb
                MM(P, PS[sb_][:, c0:512], KA[j][:, kb * 128:(kb + 1) * 128], QA[j][:, qt * 512 + c0:(qt + 1) * 512],
                   True, True, [("KA", j), ("KAaug", j), ("QA", j)], [("ps", sb_)])
                ACT(P, Eb[eb][:, c0:512], PS[sb_][:, c0:512], AF.Exp, [("ps", sb_), "abias"], [("E", eb)],
                    bias=abias[:, di:di + 1])
                if kb >= qb:
                    TT(P, "dve", Eb[eb][:, c0:c0 + 128], Eb[eb][:, c0:c0 + 128], C.trib, ALU.mult,
                       [("E", eb), "trib"], [("E", eb)])

            def emit_PV(i):
                kb, j = items[i]
                u0 = max(0, kb - qb)
                eb = slots[i]
                for u in range(u0, 4):
                    b, ap = acc_ap(j, u)
                    MM(P, ap, Eb[eb][:, u * 128:(u + 1) * 128], VP[:, kb, 0:129],
                       (kb == 0 and u % 2 == 0), (kb == qb + u),
                       [("E", eb), "VP", "VPone"], [("accb", b)], skip_group_check=True)

            LA = 2
            for i in range(len(items) + LA):
                if i < len(items):
                    emit_S(i)
                if i - LA >= 0:
                    emit_PV(i - LA)
            for u in range(4):
                p = pcnt % 2
                pcnt += 1
                b0, a0 = acc_ap(0, u)
                b1, a1 = acc_ap(1, u)
                r0 = qt * 512 + u * 128
                P.op("dve", lambda e, o_=rz[p][:, 0:1], i_=a0[:, 128:129]: e.reciprocal(out=o_, in_=i_), [("accb", b0)], [("rz", p)])
                P.op("dve", lambda e, o_=rz[p][:, 1:2], i_=a1[:, 128:129]: e.reciprocal(out=o_, in_=i_), [("accb", b1)], [("rz", p)])
                TT(P, "dve", rz[p][:, 1:2], rz[p][:, 1:2], neglam, ALU.mult, [("rz", p), "neglam"], [("rz", p)])
                TS(P, "dve", t1[p], a1[:, 0:128], rz[p][:, 1:2], None, ALU.mult, None, [("accb", b1), ("rz", p)], [("t1", p)])
                STT(P, "dve", ot[p], a0[:, 0:128], rz[p][:, 0:1], t1[p], ALU.mult, ALU.add,
                    [("accb", b0), ("rz", p), ("t1", p)], [("ot", p)])
                TT(P, "dve", sq[p], ot[p], ot[p], ALU.mult, [("ot", p)], [("sq", p)])
                RSUM(P, "dve", ss[p], sq[p], [("sq", p)], [("ss", p)])
                RSQRT(P, ss[p], ss[p], 1.0 / 128.0, [("ss", p)], [("ss", p)])
                STT(P, "dve", ob[p], ot[p], ss[p], sublnb, ALU.mult, ALU.mult, [("ot", p), ("ss", p), "sublnb"], [("ob", p)])
                P.dma(S["o"][r0:r0 + 128, h * 128:(h + 1) * 128], ob[p], reads=[("ob", p)], key=("so", p), eng="sp")


def phase_ssd(C):
    P, A, I, PS, S = C.P, C.A, C.I, C.PS, C.S
    TP, TO, TK = C.TP, C.TO, C.TK
    A.reset()
    xT = [A.alloc([128, 16, 256], BF16) for _ in range(2)]
    BT = [A.alloc([128, 4, 256], BF16) for _ in range(2)]
    CT = [A.alloc([128, 4, 256], BF16) for _ in range(2)]
    dtr = [A.alloc([128, 2, 32], F32) for _ in range(2)]
    zt = [A.alloc([128, 2, 512], F32) for _ in range(2)]
    x_tm = A.alloc([128, 2, 2048], BF16)
    B_tm = A.alloc([128, 2, 512], BF16)
    xd = A.alloc([128, 2, 2048], BF16)
    dt = A.alloc([128, 2, 32], F32)
    da = A.alloc([128, 2, 32], F32)
    cum = A.alloc([128, 2, 32], F32)
    negcum = A.alloc([128, 2, 32], F32)
    ecum = A.alloc([128, 2, 32], F32)
    dend = A.alloc([128, 2, 32], F32)
    totb = A.alloc([128, 32], F32)
    elast = A.alloc([128, 32], F32)
    cumT = A.alloc([32, 256], F32)
    stF = A.alloc([128, 2048], F32)
    stB = A.alloc([128, 2048], BF16)
    cb = [A.alloc([128, 2, 256], F32) for _ in range(4)]
    dec = [A.alloc([128, 256], F32) for _ in range(4)]
    wT = [[A.alloc([128, 256], BF16) for _ in range(2)] for _ in range(2)]
    tmp = [A.alloc([128, 512], F32) for _ in range(2)]
    yv = [A.alloc([128, 512], F32) for _ in range(2)]
    sz = [A.alloc([128, 512], F32) for _ in range(2)]
    ss = [A.alloc([128, 1], F32) for _ in range(2)]
    yo = [A.alloc([128, 512], BF16) for _ in range(2)]
    Uc = A.alloc([128, 2, 256], F32)
    mneg = A.alloc([128, 256], F32)
    OH = A.alloc([32, 32, 128], F32)
    dtbias_b = A.alloc([128, 32], F32)
    a_b = A.alloc([128, 32], F32)
    dskip_b = A.alloc([128, 32], F32)
    normw_b = A.alloc([128, 2048], F32)
    P.dma(Uc, I["k_U"], writes=["Uc"])
    P.dma(mneg, I["k_maskneg"], writes=["mneg"])
    P.dma(OH, I["k_oh"], writes=["OH"])
    P.dma(dtbias_b, I["dt_bias"].to_broadcast([128, 32]), writes=["dtbias_b"])
    P.dma(a_b, I["a_log"].to_broadcast([128, 32]), writes=["a_b"])
    P.dma(dskip_b, I["d_skip"].to_broadcast([128, 32]), writes=["dskip_b"])
    P.dma(normw_b, I["ssd_norm_w"].to_broadcast([128, 2048]), writes=["normw_b"])
    ACT(P, a_b, a_b, AF.Exp, ["a_b"], ["a_b"])
    TS(P, "dve", a_b, a_b, -1.0, None, ALU.mult, None, ["a_b"], ["a_b"])
    MEMSET(P, "dve", stF, 0.0, [], ["stF"])
    MEMSET(P, "dve", stB, 0.0, [], ["stB"])
    xsrc = S["xbcT"][0:2048, :].rearrange("(c p) t -> p c t", p=128)
    bsrc = S["xbcT"][2048:2560, :].rearrange("(c p) t -> p c t", p=128)
    csrc = S["xbcT"][2560:3072, :].rearrange("(c p) t -> p c t", p=128)
    nch = TK // 256
    npre = TP // 256

    def loads(ci):
        s = ci % 2
        t0 = ci * 256
        own = ci >= npre
        P.dma(xT[s][:, 0:8, :], xsrc[:, 0:8, t0:t0 + 256], writes=[("xT", s)], key=("xT", s))
        P.dma(xT[s][:, 8:16, :], xsrc[:, 8:16, t0:t0 + 256], writes=[("xT", s)], key=("xT", s))
        P.dma(BT[s], bsrc[:, :, t0:t0 + 256], writes=[("BT", s)], key=("BT", s))
        P.dma(dtr[s], S["dt"][t0:t0 + 256, :].rearrange("(s p) h -> p s h", p=128), writes=[("dtr", s)], key=("dtr", s))
        if own:
            P.dma(CT[s], csrc[:, :, t0:t0 + 256], writes=[("CT", s)], key=("CT", s))

    loads(0)
    gcnt = 0
    for ci in range(nch):
        if ci + 1 < nch:
            loads(ci + 1)
        s = ci % 2
        t0 = ci * 256
        own = ci >= npre
        for sub in range(2):
            for half in range(2):
                b = half
                psb = PS[b].bitcast(BF16)
                for c in range(8):
                    TR(P, psb[:, c * 128:(c + 1) * 128], xT[s][:, half * 8 + c, sub * 128:(sub + 1) * 128], C.identb,
                       [("xT", s), "identb"], [("ps", b)])
                CP(P, "act" if half == 0 else "dve", x_tm[:, sub, half * 1024:(half + 1) * 1024], psb, [("ps", b)], [("x_tm", sub)])
        psb = PS[0].bitcast(BF16)
        for sub in range(2):
            for g in range(4):
                TR(P, psb[:, (sub * 4 + g) * 128:(sub * 4 + g + 1) * 128], BT[s][:, g, sub * 128:(sub + 1) * 128], C.identb,
                   [("BT", s), "identb"], [("ps", 0)])
        CP(P, "act", B_tm.rearrange("p s n -> p (s n)"), psb, [("ps", 0)], ["B_tm"])
        TT(P, "dve", dt, dtr[s], dtbias_b.unsqueeze(1).to_broadcast([128, 2, 32]), ALU.add, [("dtr", s), "dtbias_b"], ["dt"])
        ACT(P, dt, dt, AF.Exp, ["dt"], ["dt"])
        ACT(P, dt, dt, AF.Ln, ["dt"], ["dt"], bias=1.0)
        TT(P, "dve", da, dt, a_b.unsqueeze(1).to_broadcast([128, 2, 32]), ALU.mult, ["dt", "a_b"], ["da"])
        MM(P, PS[2][:, 0:32], C.trif, da[:, 0, :], True, True, ["trif", "da"], [("ps", 2)])
        MM(P, PS[2][:, 32:64], C.onesf, da[:, 0, :], True, False, ["onesf", "da"], [("ps", 2)], skip_group_check=True)
        MM(P, PS[2][:, 32:64], C.trif, da[:, 1, :], False, True, ["trif", "da"], [("ps", 2)], skip_group_check=True)
        MM(P, PS[2][:, 64:96], C.onesf, da[:, 0, :], True, False, ["onesf", "da"], [("ps", 2)], skip_group_check=True)
        MM(P, PS[2][:, 64:96], C.onesf, da[:, 1, :], False, True, ["onesf", "da"], [("ps", 2)], skip_group_check=True)
        MM(P, PS[3][0:32, 0:256], da[:, 0, :], Uc[:, 0, :], True, False, ["da", "Uc"], [("ps", 3)])
        MM(P, PS[3][0:32, 0:256], da[:, 1, :], Uc[:, 1, :], False, True, ["da", "Uc"], [("ps", 3)])
        CP(P, "dve", cum.rearrange("p s h -> p (s h)"), PS[2][:, 0:64], [("ps", 2)], ["cum"])
        CP(P, "dve", totb, PS[2][:, 64:96], [("ps", 2)], ["totb"])
        CP(P, "dve", cumT, PS[3][0:32, 0:256], [("ps", 3)], ["cumT"])
        ACT(P, ecum, cum, AF.Exp, ["cum"], ["ecum"])
        ACT(P, elast, totb, AF.Exp, ["totb"], ["elast"])
        TS(P, "dve", negcum, cum, -1.0, None, ALU.mult, None, ["cum"], ["negcum"])
        TT(P, "dve", dend, negcum, totb.unsqueeze(1).to_broadcast([128, 2, 32]), ALU.add, ["negcum", "totb"], ["dend"])
        ACT(P, dend, dend, AF.Exp, ["dend"], ["dend"])
        TT(P, "dve", dend, dend, dt, ALU.mult, ["dend", "dt"], ["dend"])
        for sub in range(2):
            TT(P, "dve", xd[:, sub, :].rearrange("p (h q) -> p h q", q=64), x_tm[:, sub, :].rearrange("p (h q) -> p h q", q=64),
               dend[:, sub, :].unsqueeze(2).to_broadcast([128, 32, 64]), ALU.mult, [("x_tm", sub), "dend"], [("xd", sub)])
        if own:
            r0 = t0 - TP
            for g in range(4):
                b = g % 2
                for sub in range(2):
                    MM(P, PS[b][:, sub * 256:(sub + 1) * 256], BT[s][:, g, sub * 128:(sub + 1) * 128], CT[s][:, g, :], True, True,
                       [("BT", s), ("CT", s)], [("ps", b)])
                CP(P, "act" if g % 2 == 0 else "dve", cb[g].rearrange("p s t -> p (s t)"), PS[b], [("ps", b)], [("cb", g)])
            for g in range(4):
                P.dma(zt[g % 2], S["z"][r0:r0 + 256, g * 512:(g + 1) * 512].rearrange("(s p) n -> p s n", p=128),
                      writes=[("zt", g % 2)], key=("zt", g % 2))
                def seg(hh):
                    nonlocal gcnt
                    h = g * 8 + hh
                    ws = hh % 2
                    for sub in range(2):
                        tlo = 0 if sub == 0 else 128
                        ncol = 256 - tlo
                        sb_ = 4 + gcnt % 2
                        dc = gcnt % 4
                        gcnt += 1
                        MM(P, PS[sb_][:, 0:ncol], OH[:, h, :], cumT[:, tlo:256], True, False, ["OH", "cumT"], [("ps", sb_)])
                        MM(P, PS[sb_][:, 0:ncol], C.identf, mneg[:, 0:ncol], False, True, ["identf", "mneg"], [("ps", sb_)])
                        ACT(P, dec[dc][:, 0:ncol], PS[sb_][:, 0:ncol], AF.Exp, [("ps", sb_), "negcum"], [("dec", dc)],
                            bias=negcum[:, sub, h:h + 1])
                        STT(P, "dve", wT[ws][sub][:, 0:ncol], dec[dc][:, 0:ncol], dt[:, sub, h:h + 1], cb[g][:, sub, tlo:256],
                            ALU.mult, ALU.mult, [("dec", dc), "dt", ("cb", g)], [("wT", ws, sub)])

                def ymm(hh):
                    h = g * 8 + hh
                    ws = hh % 2
                    cols = slice(hh * 64, (hh + 1) * 64)
                    xc0 = x_tm[:, 0, h * 64:(h + 1) * 64]
                    xc1 = x_tm[:, 1, h * 64:(h + 1) * 64]
                    MM(P, PS[6][:, cols], wT[ws][0][:, 0:128], xc0, True, True, [("wT", ws, 0), ("x_tm", 0)], [("ps", 6)], skip_group_check=True)
                    MM(P, PS[7][:, cols], wT[ws][0][:, 128:256], xc0, True, False, [("wT", ws, 0), ("x_tm", 0)], [("ps", 7)], skip_group_check=True)
                    MM(P, PS[7][:, cols], wT[ws][1][:, 0:128], xc1, False, True, [("wT", ws, 1), ("x_tm", 1)], [("ps", 7)], skip_group_check=True)

                for hh in range(9):
                    if hh < 8:
                        seg(hh)
                    if hh >= 1:
                        ymm(hh - 1)
                for ts_ in range(2):
                    pb = 2 + ts_
                    MM(P, PS[pb], CT[s][:, g, ts_ * 128:(ts_ + 1) * 128], stB[:, g * 512:(g + 1) * 512], True, True,
                       [("CT", s), "stB"], [("ps", pb)])
                    p = ts_
                    v3 = lambda ap: ap.rearrange("p (h q) -> p h q", q=64)
                    TT(P, "dve", v3(tmp[p]), v3(PS[pb]), ecum[:, ts_, g * 8:(g + 1) * 8].unsqueeze(2).to_broadcast([128, 8, 64]), ALU.mult,
                       [("ps", pb), "ecum"], [("tmp", p)])
                    TT(P, "dve", yv[p], PS[6 + ts_], tmp[p], ALU.add, [("ps", 6 + ts_), ("tmp", p)], [("yv", p)])
                    TT(P, "dve", v3(tmp[p]), v3(x_tm[:, ts_, g * 512:(g + 1) * 512]),
                       dskip_b[:, g * 8:(g + 1) * 8].unsqueeze(2).to_broadcast([128, 8, 64]), ALU.mult,
                       [("x_tm", ts_), "dskip_b", ("tmp", p)], [("tmp", p)])
                    TT(P, "dve", yv[p], yv[p], tmp[p], ALU.add, [("yv", p), ("tmp", p)], [("yv", p)])
                    ACT(P, sz[p], zt[g % 2][:, ts_, :], AF.Silu, [("zt", g % 2)], [("sz", p)])
                    TT(P, "dve", yv[p], yv[p], sz[p], ALU.mult, [("yv", p), ("sz", p)], [("yv", p)])
                    TT(P, "dve", tmp[p], yv[p], yv[p], ALU.mult, [("yv", p)], [("tmp", p)])
                    RSUM(P, "dve", ss[p], tmp[p], [("tmp", p)], [("ss", p)])
                    RSQRT(P, ss[p], ss[p], 1.0 / 512.0, [("ss", p)], [("ss", p)])
                    STT(P, "dve", yo[p], yv[p], ss[p], normw_b[:, g * 512:(g + 1) * 512], ALU.mult, ALU.mult,
                        [("yv", p), ("ss", p), "normw_b"], [("yo", p)])
                    P.dma(S["yssd"][r0 + ts_ * 128:r0 + (ts_ + 1) * 128, g * 512:(g + 1) * 512], yo[p], reads=[("yo", p)],
                          key=("so", p), eng="act")
        for g in range(4):
            b = g % 2
            MM(P, PS[b], B_tm[:, 0, g * 128:(g + 1) * 128], xd[:, 0, g * 512:(g + 1) * 512], True, False, ["B_tm", ("xd", 0)], [("ps", b)])
            MM(P, PS[b], B_tm[:, 1, g * 128:(g + 1) * 128], xd[:, 1, g * 512:(g + 1) * 512], False, True, ["B_tm", ("xd", 1)], [("ps", b)])
            sv = stF[:, g * 512:(g + 1) * 512]
            TT(P, "dve", sv.rearrange("p (h q) -> p h q", q=64), sv.rearrange("p (h q) -> p h q", q=64),
               elast[:, g * 8:(g + 1) * 8].unsqueeze(2).to_broadcast([128, 8, 64]), ALU.mult, ["stF", "elast"], ["stF"])
            TT(P, "dve", sv, sv, PS[b], ALU.add, ["stF", ("ps", b)], ["stF"])
            if ci == npre - 1:
                TS(P, "dve", sv, sv, C.flag, None, ALU.mult, None, ["stF", "flag"], ["stF"])
            CP(P, "dve", stB[:, g * 512:(g + 1) * 512], sv, ["stF"], ["stB"])


def _layernorm(P, eng2, sf, gb, bb, outap, tag, bufs):
    m, sq, ss = bufs
    RSUM(P, "dve", m, sf, [(tag, "sf")], [(tag, "m")])
    TS(P, "dve", m, m, -1.0 / 1024.0, None, ALU.mult, None, [(tag, "m")], [(tag, "m")])
    TS(P, "dve", sf, sf, m, None, ALU.add, None, [(tag, "sf"), (tag, "m")], [(tag, "sf")])
    TT(P, eng2, sq, sf, sf, ALU.mult, [(tag, "sf")], [(tag, "sq")])
    RSUM(P, "dve", ss, sq, [(tag, "sq")], [(tag, "ss")])
    RSQRT(P, ss, ss, 1.0 / 1024.0, [(tag, "ss")], [(tag, "ss")])
    STT(P, "dve", sf, sf, ss, gb, ALU.mult, ALU.mult, [(tag, "sf"), (tag, "ss"), "lnconst"], [(tag, "sf")])
    TT(P, eng2, outap, sf, bb, ALU.add, [(tag, "sf"), "lnconst"], [(tag, "out")])


def phase_merge(C):
    P, A, I, PS, S = C.P, C.A, C.I, C.PS, C.S
    TP, TO = C.TP, C.TO
    A.reset()
    wa = A.alloc([128, 8, 1024], BF16)
    ws = A.alloc([128, 16, 1024], BF16)
    wo = A.alloc([128, 8, 1024], BF16)
    wst = [A.alloc([128, 8, 256], F32) for _ in range(2)]
    lng = A.alloc([128, 1024], F32)
    lnb = A.alloc([128, 1024], F32)
    P.dma(lng, I["ln1_g"].to_broadcast([128, 1024]), writes=["lnconst"])
    P.dma(lnb, I["ln1_b"].to_broadcast([128, 1024]), writes=["lnconst"])
    k = 0
    for (wsrc, wdst, nk) in ((I["w_attn_branch"], wa, 8), (I["w_ssd_branch"], ws, 16), (I["w_out"], wo, 8)):
        wv = wsrc.rearrange("(c p) n -> p c n", p=128)
        for c0 in range(0, nk, 8):
            for nh in range(4):
                s = k % 2
                k += 1
                P.dma(wst[s], wv[:, c0:c0 + 8, nh * 256:(nh + 1) * 256], writes=[("wst", s)], key=("w", s))
                CP(P, "dve" if k % 2 else "dve", wdst[:, c0:c0 + 8, nh * 256:(nh + 1) * 256], wst[s], [("wst", s)], ["wts"])
    ob = [A.alloc([128, 1024], BF16) for _ in range(2)]
    yb = [A.alloc([128, 2048], BF16) for _ in range(2)]
    gt = [A.alloc([128, 2048], F32) for _ in range(2)]
    xt = [A.alloc([128, 1024], F32) for _ in range(2)]
    oT = A.alloc([128, 8, 128], BF16)
    yT = A.alloc([128, 16, 128], BF16)
    uT = A.alloc([128, 8, 128], BF16)
    hT2 = [A.alloc([128, 8, 128], BF16) for _ in range(2)]
    t1 = A.alloc([128, 1024], F32)
    t2 = A.alloc([128, 1024], F32)
    ub = A.alloc([128, 1024], BF16)
    sf = A.alloc([128, 1024], F32)
    sq = A.alloc([128, 1024], F32)
    x1 = [A.alloc([128, 1024], F32) for _ in range(2)]
    hb = A.alloc([128, 1024], BF16)
    m = A.alloc([128, 1], F32)
    ss = A.alloc([128, 1], F32)
    ntt = TO // 128

    def loads(tt):
        s = tt % 2
        r0 = tt * 128
        P.dma(ob[s], S["o"][r0:r0 + 128, :], writes=[("ob", s)], key=("ob", s))
        P.dma(yb[s], S["yssd"][r0:r0 + 128, :], writes=[("yb", s)], key=("yb", s))
        P.dma(gt[s], S["g"][r0:r0 + 128, :], writes=[("gt", s)], key=("gt", s))
        P.dma(xt[s], I["xo"][r0:r0 + 128, :], writes=[("xt", s)], key=("xt", s))

    loads(0)
    for tt in range(ntt):
        if tt + 1 < ntt:
            loads(tt + 1)
        s = tt % 2
        r0 = tt * 128
        pb0 = PS[0].bitcast(BF16)
        pb1 = PS[1].bitcast(BF16)
        pb2 = PS[2].bitcast(BF16)
        for c in range(8):
            TR(P, pb0[:, c * 128:(c + 1) * 128], ob[s][:, c * 128:(c + 1) * 128], C.identb, [("ob", s), "identb"], [("ps", 0)])
        CP(P, "act", oT.rearrange("p c t -> p (c t)"), pb0, [("ps", 0)], ["oT"])
        for c in range(16):
            pbx = pb1 if c < 8 else pb2
            TR(P, pbx[:, (c % 8) * 128:(c % 8 + 1) * 128], yb[s][:, c * 128:(c + 1) * 128], C.identb, [("yb", s), "identb"], [("ps", 1 + c // 8)])
        CP(P, "dve", yT[:, 0:8, :].rearrange("p c t -> p (c t)"), pb1, [("ps", 1)], ["yT"])
        CP(P, "act", yT[:, 8:16, :].rearrange("p c t -> p (c t)"), pb2, [("ps", 2)], ["yT"])
        for nh in range(2):
            for c in range(8):
                MM(P, PS[3 + nh], oT[:, c, :], wa[:, c, nh * 512:(nh + 1) * 512], c == 0, c == 7, ["oT", "wts"], [("ps", 3 + nh)])
            for c in range(16):
                MM(P, PS[5 + nh], yT[:, c, :], ws[:, c, nh * 512:(nh + 1) * 512], c == 0, c == 15, ["yT", "wts"], [("ps", 5 + nh)])
            cs = slice(nh * 512, (nh + 1) * 512)
            TT(P, "dve", t1[:, cs], PS[3 + nh], gt[s][:, nh * 512:(nh + 1) * 512], ALU.mult, [("ps", 3 + nh), ("gt", s)], [("t1", nh)])
            TT(P, "dve", t2[:, cs], PS[5 + nh], gt[s][:, 1024 + nh * 512:1024 + (nh + 1) * 512], ALU.mult, [("ps", 5 + nh), ("gt", s)], [("t2", nh)])
            TT(P, "dve", ub[:, cs], t1[:, cs], t2[:, cs], ALU.add, [("t1", nh), ("t2", nh)], [("ub", nh)])
        for c in range(8):
            TR(P, pb0[:, c * 128:(c + 1) * 128], ub[:, c * 128:(c + 1) * 128], C.identb, [("ub", c // 4), "identb"], [("ps", 0)])
        CP(P, "act", uT.rearrange("p c t -> p (c t)"), pb0, [("ps", 0)], ["uT"])
        for nh in range(2):
            for c in range(8):
                MM(P, PS[3 + nh], uT[:, c, :], wo[:, c, nh * 512:(nh + 1) * 512], c == 0, c == 7, ["uT", "wts"], [("ps", 3 + nh)])
            cs = slice(nh * 512, (nh + 1) * 512)
            TT(P, "dve", sf[:, cs], PS[3 + nh], C.modb[:, 2 * D + nh * 512:2 * D + (nh + 1) * 512], ALU.mult,
               [("ps", 3 + nh), "modb", ("ln1", "out")], [("ln1", "sfh", nh)])
        STT(P, "dve", sf, xt[s], ALPHA, sf, ALU.mult, ALU.add, [("xt", s), ("ln1", "sfh", 0), ("ln1", "sfh", 1)], [("ln1", "sf")])
        xo_ = x1[s]
        _layernorm(P, "dve", sf, lng, lnb, xo_, "ln1", (m, sq, ss))
        P.dma(S["x1"][r0:r0 + 128, :], xo_, reads=[("ln1", "out")], writes=[("x1st", s)], key=("x1st", s), eng="act")
        TT(P, "dve", t1, xo_, C.modb[:, 4 * D:5 * D], ALU.mult, [("ln1", "out"), "modb", ("t1", 0), ("t1", 1)], [("t1", 0), ("t1", 1)])
        TT(P, "dve", hb, t1, C.modb[:, 3 * D:4 * D], ALU.add, [("t1", 0), ("t1", 1), "modb"], ["hb"])
        for c in range(8):
            TR(P, pb1[:, c * 128:(c + 1) * 128], hb[:, c * 128:(c + 1) * 128], C.identb, ["hb", "identb"], [("ps", 1)])
        CP(P, "act", hT2[s].rearrange("p c t -> p (c t)"), pb1, [("ps", 1)], [("hT2", s)])
        P.dma(S["h2T"][:, :, r0:r0 + 128], hT2[s], reads=[("hT2", s)], key=("h2st", s), eng="act")


def phase_peer(C):
    P, A, I, PS, S = C.P, C.A, C.I, C.PS, C.S
    TP, TO = C.TP, C.TO
    A.reset()
    ust = [A.alloc([128, 1024], F32) for _ in range(2)]
    ubf = [A.alloc([128, 1024], BF16) for _ in range(2)]
    utT = [A.alloc([128, 8, 128], BF16) for _ in range(2)]
    vst = [A.alloc([128, 1024], F32) for _ in range(2)]
    vbf = [A.alloc([128, 1024], BF16) for _ in range(2)]
    Uv = I["peer_u"].rearrange("(i j) d -> j i d", j=128)
    Vv = I["peer_v"].rearrange("(i j) d -> j i d", j=128)
    for j in range(128):
        s = j % 2
        P.dma(ust[s], Uv[j], writes=[("ust", s)], key=("ust", s))
        P.dma(vst[s], Vv[j], writes=[("vst", s)], key=("vst", s))
        CP(P, "dve", ubf[s], ust[s], [("ust", s)], [("ubf", s)])
        psb = PS[s].bitcast(BF16)
        for c in range(8):
            TR(P, psb[:, c * 128:(c + 1) * 128], ubf[s][:, c * 128:(c + 1) * 128], C.identb, [("ubf", s), "identb"], [("ps", s)])
        CP(P, "act", utT[s].rearrange("p c i -> p (c i)"), psb, [("ps", s)], [("utT", s)])
        P.dma(S["UT"][j], utT[s], reads=[("utT", s)], key=("sut", s), eng="act")
        CP(P, "dve", vbf[s], vst[s], [("vst", s)], [("vbf", s)])
        P.dma(S["VB"][j], vbf[s], reads=[("vbf", s)], key=("svb", s), eng="act")
    P.barrier()
    A.reset()
    wq = A.alloc([128, 8, 2048], BF16)
    skr = A.alloc([128, 2, 128], F32)
    skT = A.alloc([128, 2, 128], F32)
    iota = A.alloc([128, 128], F32)
    iotab = A.alloc([128, 128], BF16)
    Wt = A.alloc([128, 128, 256], BF16)
    h2p = A.alloc([128, 8, 256], BF16)
    base_off = A.off
    wst = [A.alloc([128, 8, 256], F32) for _ in range(2)]
    wv = I["peer_w_query"].rearrange("(c p) n -> p c n", p=128)
    for k in range(8):
        s = k % 2
        P.dma(wst[s], wv[:, :, k * 256:(k + 1) * 256], writes=[("wst", s)], key=("w", s))
        CP(P, "dve" if k % 2 else "dve", wq[:, :, k * 256:(k + 1) * 256], wst[s], [("wst", s)], ["wq"])
    P.dma(skr, I["peer_sub_keys"].rearrange("j k d -> k j d"), writes=["skr"])
    for j in range(2):
        TR(P, PS[6][:, j * 128:(j + 1) * 128], skr[:, j, :], C.identf, ["skr", "identf"], [("ps", 6)])
    CP(P, "dve", skT.rearrange("p j k -> p (j k)"), PS[6][:, 0:256], [("ps", 6)], ["skT"])
    P.dma(iota, I["k_iota"], writes=["iota"])
    CP(P, "dve", iotab, iota, ["iota"], ["iotab"])
    P.barrier()
    A.reset(base_off)
    qT = A.alloc([128, 16, 256], F32)
    sc = A.alloc([128, 2048], F32)
    vals = A.alloc([128, 16, 16], F32)
    idxu = A.alloc([128, 16, 16], U32)
    idxf = A.alloc([128, 16, 16], F32)
    tmpk2 = [A.alloc([128, 128], F32) for _ in range(2)]
    cand = A.alloc([128, 8, 16, 16], F32)
    tops = A.alloc([128, 8, 16], F32)
    posu = A.alloc([128, 8, 16], U32)
    au = A.alloc([128, 8, 16], U32)
    bu = A.alloc([128, 8, 16], U32)
    af = A.alloc([128, 8, 16], F32)
    bf_ = A.alloc([128, 8, 16], F32)
    tmpc2 = [A.alloc([128, 256], F32) for _ in range(2)]
    eq = cand
    selc = A.alloc([128, 3, 128], F32)
    ee = A.alloc([128, 8, 16], F32)
    Zs = A.alloc([128, 8], F32)
    selT = A.alloc([128, 3, 128], BF16)
    OI = [A.alloc([128, 32, 128], BF16)]
    GJ = [A.alloc([128, 32, 128], BF16)]
    A.reset(base_off)
    lng = A.alloc([128, 1024], F32)
    lnb = A.alloc([128, 1024], F32)
    NB = 4
    ut = [A.alloc([128, 8, 128], BF16) for _ in range(NB)]
    vb = [A.alloc([128, 1024], BF16) for _ in range(NB)]
    gs = [A.alloc([128, 256], F32) for _ in range(2)]
    wg = [A.alloc([128, 256], BF16) for _ in range(2)]
    x1t = [A.alloc([128, 1024], F32) for _ in range(2)]
    sf = A.alloc([128, 1024], F32)
    sq = A.alloc([128, 1024], F32)
    ot = [A.alloc([128, 1024], F32) for _ in range(2)]
    m = A.alloc([128, 1], F32)
    ss = A.alloc([128, 1], F32)
    vals4 = vals.rearrange("p (h j) k -> p h j k", j=2)
    idxf4 = idxf.rearrange("p (h j) k -> p h j k", j=2)
    iota16b = iota[:, 0:16].unsqueeze(1).unsqueeze(1).to_broadcast([128, 8, 16, 16])
    npass = TO // 256
    ALLPOS = [("posu", h_, k_) for h_ in range(8) for k_ in range(2)]
    ALLTOPS = [("tops", h_, k_) for h_ in range(8) for k_ in range(2)]
    ecnt = 0
    for ps_ in range(npass):
        t0 = ps_ * 256
        P.dma(h2p, S["h2T"][:, :, t0:t0 + 256], writes=["h2p"], key="h2p")
        for hj in range(16):
            b = 4 + hj % 2
            for c in range(8):
                MM(P, PS[b][:, 0:256], wq[:, c, hj * 128:(hj + 1) * 128], h2p[:, c, :], c == 0, c == 7, ["wq", "h2p"], [("ps", b)])
            CP(P, "act" if hj % 2 == 0 else "dve", qT[:, hj, :], PS[b][:, 0:256], [("ps", b)], ["qT"])
        for tt in range(2):
            for hj in range(16):
                b = hj // 4
                MM(P, PS[b][:, (hj % 4) * 128:(hj % 4 + 1) * 128], qT[:, hj, tt * 128:(tt + 1) * 128], skT[:, hj % 2, :], True, True,
                   ["qT", "skT"], [("ps", b)], skip_group_check=True)
            for b in range(4):
                CP(P, "act" if b % 2 == 0 else "dve", sc[:, b * 512:(b + 1) * 512], PS[b], [("ps", b)], ["sc"])
            for hj in range(16):
                v = sc[:, hj * 128:(hj + 1) * 128]
                tk = tmpk2[hj % 2]
                P.op("dve", lambda e, o_=vals[:, hj, 0:8], i_=v: e.max(out=o_, in_=i_), ["sc"], [("vals", hj, 0)])
                P.op("dve", lambda e, o_=idxu[:, hj, 0:8], m_=vals[:, hj, 0:8], i_=v: e.max_index(out=o_, in_max=m_, in_values=i_), ["sc", ("vals", hj, 0)], [("idxu", hj, 0)])
                P.op("dve", lambda e, o_=tk, m_=vals[:, hj, 0:8], i_=v: e.match_replace(out=o_, in_to_replace=m_, in_values=i_, imm_value=-1e30),
                     ["sc", ("vals", hj, 0)], [("tmpk", hj % 2)])
                P.op("dve", lambda e, o_=vals[:, hj, 8:16], i_=tk: e.max(out=o_, in_=i_), [("tmpk", hj % 2)], [("vals", hj, 1)])
                P.op("dve", lambda e, o_=idxu[:, hj, 8:16], m_=vals[:, hj, 8:16], i_=tk: e.max_index(out=o_, in_max=m_, in_values=i_),
                     [("tmpk", hj % 2), ("vals", hj, 1)], [("idxu", hj, 1)])
            allidx = [("idxu", hj_, k_) for hj_ in range(16) for k_ in range(2)]
            allvals = [("vals", hj_, k_) for hj_ in range(16) for k_ in range(2)]
            CP(P, "dve", idxf, idxu, allidx, ["idxf"])
            TT(P, "dve", cand, vals4[:, :, 0, :].unsqueeze(3).to_broadcast([128, 8, 16, 16]),
               vals4[:, :, 1, :].unsqueeze(2).to_broadcast([128, 8, 16, 16]), ALU.add, allvals, ["cand"])
            for h in range(8):
                cv = cand[:, h].rearrange("p a b -> p (a b)")
                tc_ = tmpc2[h % 2]
                P.op("dve", lambda e, o_=tops[:, h, 0:8], i_=cv: e.max(out=o_, in_=i_), ["cand"], [("tops", h, 0)])
                P.op("dve", lambda e, o_=posu[:, h, 0:8], m_=tops[:, h, 0:8], i_=cv: e.max_index(out=o_, in_max=m_, in_values=i_), ["cand", ("tops", h, 0)], [("posu", h, 0)])
                P.op("dve", lambda e, o_=tc_, m_=tops[:, h, 0:8], i_=cv: e.match_replace(out=o_, in_to_replace=m_, in_values=i_, imm_value=-1e30),
                     ["cand", ("tops", h, 0)], [("tmpc", h % 2)])
                P.op("dve", lambda e, o_=tops[:, h, 8:16], i_=tc_: e.max(out=o_, in_=i_), [("tmpc", h % 2)], [("tops", h, 1)])
                P.op("dve", lambda e, o_=posu[:, h, 8:16], m_=tops[:, h, 8:16], i_=tc_: e.max_index(out=o_, in_max=m_, in_values=i_),
                     [("tmpc", h % 2), ("tops", h, 1)], [("posu", h, 1)])
            P.op("dve", lambda e: e.tensor_single_scalar(out=au, in_=posu, scalar=4, op=ALU.logical_shift_right), ALLPOS, ["au"])
            P.op("dve", lambda e: e.tensor_single_scalar(out=bu, in_=posu, scalar=15, op=ALU.bitwise_and), ALLPOS, ["bu"])
            CP(P, "dve", af, au, ["au"], ["af"])
            CP(P, "dve", bf_, bu, ["bu"], ["bf"])
            for (srcf, jj, dst) in ((af, 0, 0), (bf_, 1, 1)):
                TT(P, "dve", eq, iota16b, srcf.unsqueeze(3).to_broadcast([128, 8, 16, 16]), ALU.is_equal, ["iota", "af", "bf"] + ALLTOPS + ALLPOS, ["cand"])
                TT(P, "dve", eq, eq, idxf4[:, :, jj, :].unsqueeze(2).to_broadcast([128, 8, 16, 16]), ALU.mult, ["cand", "idxf"], ["cand"])
                RSUM(P, "dve", selc[:, dst, :].rearrange("p (h s) -> p h s", s=16), eq, ["cand"], ["selc"])
            TT(P, "dve", ee, tops, tops[:, :, 0:1].to_broadcast([128, 8, 16]), ALU.subtract, ALLTOPS, ["ee"])
            ACT(P, ee, ee, AF.Exp, ["ee"], ["ee"])
            RSUM(P, "dve", Zs, ee, ["ee"], ["Zs"])
            P.op("dve", lambda e: e.reciprocal(out=Zs, in_=Zs), ["Zs"], ["Zs"])
            TT(P, "dve", selc[:, 2, :].rearrange("p (h s) -> p h s", s=16), ee, Zs.unsqueeze(2).to_broadcast([128, 8, 16]), ALU.mult,
               ["ee", "Zs"], ["selc"])
            for k in range(3):
                TR(P, PS[6][:, k * 128:(k + 1) * 128], selc[:, k, :], C.identf, ["selc", "identf"], [("ps", 6)])
            CP(P, "act", selT.rearrange("p k t -> p (k t)"), PS[6][:, 0:384], [("ps", 6)], ["selT"])
            for half in range(4):
                es = 0
                tsl = slice(half * 32, (half + 1) * 32)
                iob = iotab.unsqueeze(1).to_broadcast([128, 32, 128])
                TT(P, "dve", OI[es], iob, selT[:, 0, tsl].unsqueeze(2).to_broadcast([128, 32, 128]), ALU.is_equal,
                   ["iotab", "selT"], [("OI", es)])
                TT(P, "dve", GJ[es], iob, selT[:, 1, tsl].unsqueeze(2).to_broadcast([128, 32, 128]), ALU.is_equal,
                   ["iotab", "selT"], [("GJ", es)])
                TT(P, "dve", GJ[es], GJ[es], selT[:, 2, tsl].unsqueeze(2).to_broadcast([128, 32, 128]), ALU.mult,
                   [("GJ", es), "selT"], [("GJ", es)])
                for q4 in range(8):
                    b = 6 + q4 % 2
                    for k in range(4):
                        t = q4 * 4 + k
                        MM(P, PS[b][:, k * 128:(k + 1) * 128], OI[es][:, t, :], GJ[es][:, t, :], True, True,
                           [("OI", es), ("GJ", es)], [("ps", b)], skip_group_check=True)
                    tok0 = tt * 128 + half * 32 + q4 * 4
                    CP(P, "act" if q4 % 2 == 0 else "dve", Wt[:, :, tok0:tok0 + 4], PS[b].rearrange("p (t j) -> p j t", t=4),
                       [("ps", b)], ["Wt"])
        P.barrier()
        P.dma(lng, I["ln2_g"].to_broadcast([128, 1024]), writes=["lnconst"])
        P.dma(lnb, I["ln2_b"].to_broadcast([128, 1024]), writes=["lnconst"])

        def eload(j):
            s = j % NB
            P.dma(ut[s], S["UT"][j], writes=[("ut", s)], key=("ut", s))
            P.dma(vb[s], S["VB"][j], writes=[("vb", s)], key=("vb", s))

        eload(0)
        eload(1)

        def escore(j):
            s = j % NB
            b = 4 + j % 2
            g2 = j % 2
            for c in range(8):
                MM(P, PS[b][:, 0:256], ut[s][:, c, :], h2p[:, c, :], c == 0, c == 7, [("ut", s), "h2p"], [("ps", b)])
            ACT(P, gs[g2], PS[b][:, 0:256], AF.Gelu, [("ps", b)], [("gs", g2)])
            TT(P, "dve", wg[g2], gs[g2], Wt[:, j, :], ALU.mult, [("gs", g2), "Wt"], [("wg", g2)])

        def eacc(j):
            s = j % NB
            g2 = j % 2
            for tsub in range(2):
                for dh in range(2):
                    ab = tsub * 2 + dh
                    MM(P, PS[ab], wg[g2][:, tsub * 128:(tsub + 1) * 128], vb[s][:, dh * 512:(dh + 1) * 512], j == 0, j == 127,
                       [("wg", g2), ("vb", s)], [("ps", ab)])

        for j in range(129):
            if j < 128:
                if j + 2 < 128:
                    eload(j + 2)
                escore(j)
            if j >= 1:
                eacc(j - 1)
        for tsub in range(2):
            r0 = t0 + tsub * 128
            xs_ = x1t[tsub]
            P.dma(xs_, S["x1"][r0:r0 + 128, :], writes=[("x1t", tsub)], key=("x1t", tsub))
            for dh in range(2):
                ab = tsub * 2 + dh
                TT(P, "dve", sf[:, dh * 512:(dh + 1) * 512], PS[ab], C.modb[:, 5 * D + dh * 512:5 * D + (dh + 1) * 512], ALU.mult,
                   [("ps", ab), "modb", ("ln2", "out")], [("ln2", "sfh", dh)])
            STT(P, "dve", sf, xs_, ALPHA, sf, ALU.mult, ALU.add, [("x1t", tsub), ("ln2", "sfh", 0), ("ln2", "sfh", 1)], [("ln2", "sf")])
            _layernorm(P, "dve", sf, lng, lnb, ot[tsub], "ln2", (m, sq, ss))
            P.dma(C.out[r0:r0 + 128, :], ot[tsub], reads=[("ln2", "out")], writes=[("ost", tsub)], key=("ost", tsub), eng="act")
        P.barrier()


def make_consts(TP, TO):
    NKB = (TP + TO) // 128
    ND = NKB + 4
    p = np.arange(128)
    tri = (p[:, None] <= p[None, :]).astype(np.float32)
    U = np.zeros((128, 2, 256), np.float32)
    U[:, 0, 0:128] = tri
    U[:, 0, 128:256] = 1.0
    U[:, 1, 128:256] = tri
    maskneg = np.zeros((128, 256), np.float32)
    maskneg[:, 0:128] = np.where(p[:, None] <= p[None, :], 0.0, NEG)
    oh = np.zeros((32, 32, 128), np.float32)
    for h in range(32):
        oh[h, h, :] = 1.0
    slopes = np.array([2.0 ** (-(i + 1)) for i in range(8)], np.float64)
    q = np.arange(512)
    aug = np.zeros((8, 2, 512), np.float32)
    aug[:, 0, :] = -slopes[:, None] * (128.0 * (q // 128))[None, :]
    aug[:, 1, :] = -slopes[:, None] * (q % 128)[None, :]
    di = np.arange(ND)
    abias = slopes[:, None, None] * (p[None, :, None] + 128.0 * (di[None, None, :] - (NKB - 4)))
    return {
        "k_ident": np.eye(128, dtype=np.float32),
        "k_tri": tri,
        "k_U": U,
        "k_maskneg": maskneg,
        "k_oh": oh,
        "k_aug": aug.astype(ml_dtypes.bfloat16),
        "k_abias": abias.astype(np.float32),
        "k_iota": np.tile(np.arange(128, dtype=np.float32)[None, :], (128, 1)),
    }


def make_in_maps(inputs, TP, TO, cores, nexp=16384):
    consts = make_consts(TP, TO)
    f = lambda a: np.ascontiguousarray(np.asarray(a, dtype=np.float32))
    shared = {
        "w_ada": f(inputs["w_ada"][0]), "b_ada": f(inputs["b_ada"][0])[None, :],
        "w_in": f(inputs["w_in"][0]), "conv_w": f(inputs["conv_w"][0]), "conv_b": f(inputs["conv_b"][0])[None, :],
        "dt_bias": f(inputs["dt_bias"][0])[None, :], "a_log": f(inputs["a_log"][0])[None, :],
        "d_skip": f(inputs["d_skip"][0])[None, :], "ssd_norm_w": f(inputs["ssd_norm_w"][0])[None, :],
        "lambda_q1": f(inputs["lambda_q1"][0])[None, :], "lambda_k1": f(inputs["lambda_k1"][0])[None, :],
        "lambda_q2": f(inputs["lambda_q2"][0])[None, :], "lambda_k2": f(inputs["lambda_k2"][0])[None, :],
        "da_subln_w": f(inputs["da_subln_w"][0])[None, :],
        "w_attn_branch": f(inputs["w_attn_branch"][0]), "w_ssd_branch": f(inputs["w_ssd_branch"][0]),
        "w_out": f(inputs["w_out"][0]), "ln1_g": f(inputs["ln1_g"][0])[None, :], "ln1_b": f(inputs["ln1_b"][0])[None, :],
        "peer_w_query": f(inputs["peer_w_query"][0]), "peer_sub_keys": f(inputs["peer_sub_keys"][0]),
        "peer_u": f(inputs["peer_u"][0][:nexp]), "peer_v": f(inputs["peer_v"][0][:nexp]),
        "ln2_g": f(inputs["ln2_g"][0])[None, :], "ln2_b": f(inputs["ln2_b"][0])[None, :],
    }
    shared.update(consts)
    x = np.asarray(inputs["x"], dtype=np.float32)
    c = np.asarray(inputs["c"], dtype=np.float32)
    maps = []
    for (b, half) in cores:
        m = dict(shared)
        if half == 0:
            m["xp"] = np.zeros((TP, D), np.float32)
            m["xo"] = np.ascontiguousarray(x[b, 0:TO])
            m["flag"] = np.zeros((128, 1), np.float32)
        else:
            m["xp"] = np.ascontiguousarray(x[b, 0:TP])
            m["xo"] = np.ascontiguousarray(x[b, TP:TP + TO])
            m["flag"] = np.ones((128, 1), np.float32)
        m["c_col"] = np.ascontiguousarray(c[b].reshape(8, 128).T)
        maps.append(m)
    return maps


def kernel(**inputs):
    x = np.asarray(inputs["x"])
    B, SEQ, _ = x.shape
    TP = TO = SEQ // 2
    nc = build(TP, TO)
    cores = [(b, h) for b in range(B) for h in range(2)]
    maps = make_in_maps(inputs, TP, TO, cores)
    res = run_bass_kernel_spmd(nc, maps, core_ids=list(range(len(cores))))
    out = np.zeros((B, SEQ, D), np.float32)
    for i, (b, h) in enumerate(cores):
        out[b, h * TO:(h + 1) * TO] = np.asarray(res.results[i]["out"], dtype=np.float32)
    return out
```
